# Optimizing a Trainium2 kernel written in Bass

```python
import math
import jax, jax.numpy as jnp
from jax import lax
import numpy as np

D_MODEL = 4096
BATCH = 1
SEQ = 16384
DEPTH = 4

N_EVEN = (DEPTH + 1) // 2
N_ODD = DEPTH // 2
N_MOD = 6
CONV_CH = D_MODEL // 2
CONV_K = 3
ATTN_HEAD_DIM = 128
ATTN_HEADS = (D_MODEL // 2) // (2 * ATTN_HEAD_DIM)
ATTN_WIDTH = ATTN_HEADS * 2 * ATTN_HEAD_DIM
MIX_WIDTH = CONV_CH + ATTN_WIDTH
IN_PROJ_WIDTH = 3 * CONV_CH + 3 * ATTN_WIDTH
Q_BLOCK = 128
S5_GROUP = 16
S5_STATE = 64
S5_GROUPS = D_MODEL // S5_GROUP
S5_CHUNK = 128
DT_MIN = 0.001
DT_MAX = 0.1
FFN_HIDDEN = -(-8 * D_MODEL // (3 * 256)) * 256
NORM_EPS = 1e-6
SUBLN_EPS = 1e-5

kernel_name = 'hybrid_conv_diffattn_s5_adaln_trunk'


def rmsnorm(x, g, eps=NORM_EPS):
    xf = x.astype(jnp.float32)
    y = xf * lax.rsqrt(jnp.mean(xf * xf, axis=-1, keepdims=True) + eps)
    return (y * g.astype(jnp.float32)).astype(x.dtype)


def modulate(h, shift, scale):
    return h * (1 + scale[:, None, :]) + shift[:, None, :]


def causal_depthwise_conv(u, w):
    k_width, ch = w.shape
    return lax.conv_general_dilated(
        u, w[:, None, :].astype(u.dtype), window_strides=(1,),
        padding=[(k_width - 1, 0)], dimension_numbers=('NWC', 'WIO', 'NWC'),
        feature_group_count=ch)


def diff_attention(q, k, v, lam, lam_init, subln_g):
    bsz, seq = q.shape[0], q.shape[1]
    nb = seq // Q_BLOCK
    q = q * jnp.asarray(ATTN_HEAD_DIM ** -0.5, q.dtype)
    q_blocks = jnp.moveaxis(q.reshape(bsz, nb, Q_BLOCK, *q.shape[2:]), 1, 0)
    key_pos = jnp.arange(seq)

    def block(args):
        q_blk, blk = args
        s = jnp.einsum('bqhmd,bkhmd->bhmqk', q_blk, k).astype(jnp.float32)
        q_pos = blk * Q_BLOCK + jnp.arange(Q_BLOCK)
        s = jnp.where(key_pos[None, :] <= q_pos[:, None], s, -jnp.inf)
        p = jax.nn.softmax(s, axis=-1)
        w = p[:, :, 0] - lam * p[:, :, 1]
        return jnp.einsum('bhqk,bkhe->bqhe', w.astype(v.dtype), v)

    o = lax.map(block, (q_blocks, jnp.arange(nb)))
    o = jnp.moveaxis(o, 0, 1).reshape(bsz, seq, ATTN_HEADS, 2 * ATTN_HEAD_DIM)
    o = rmsnorm(o, subln_g, SUBLN_EPS) * (1.0 - lam_init)
    return o.reshape(bsz, seq, ATTN_WIDTH)


def conv_diffattn_mixer(h, w_in, conv_w, lq1, lk1, lq2, lk2, subln_g, w_out, lam_init):
    bsz, seq, _ = h.shape
    z = h @ w_in
    splits = [CONV_CH, 2 * CONV_CH, 3 * CONV_CH,
              3 * CONV_CH + ATTN_WIDTH, 3 * CONV_CH + 2 * ATTN_WIDTH]
    gate_b, gate_c, x_in, q, k, v = jnp.split(z, splits, axis=-1)
    y_conv = gate_b * causal_depthwise_conv(gate_c * x_in, conv_w)
    f32 = jnp.float32
    lam = (jnp.exp(jnp.sum(lq1.astype(f32) * lk1.astype(f32)))
           - jnp.exp(jnp.sum(lq2.astype(f32) * lk2.astype(f32))) + lam_init)
    q = q.reshape(bsz, seq, ATTN_HEADS, 2, ATTN_HEAD_DIM)
    k = k.reshape(bsz, seq, ATTN_HEADS, 2, ATTN_HEAD_DIM)
    v = v.reshape(bsz, seq, ATTN_HEADS, 2 * ATTN_HEAD_DIM)
    y_attn = diff_attention(q, k, v, lam, lam_init, subln_g)
    return jnp.concatenate([y_conv, y_attn.astype(y_conv.dtype)], axis=-1) @ w_out


def s5_ssm(u, a_re, a_im, log_dt, b_re, b_im, c_re, c_im, d_skip):
    f32 = jnp.float32
    bsz, seq, _ = u.shape
    lam_re = jnp.minimum(a_re.astype(f32), -1e-4)
    lam_im = a_im.astype(f32)
    dt = jnp.exp(log_dt.astype(f32))[:, None]
    mag = jnp.exp(lam_re * dt)
    ab_re = mag * jnp.cos(lam_im * dt)
    ab_im = mag * jnp.sin(lam_im * dt)
    den = lam_re * lam_re + lam_im * lam_im
    nr, ni = ab_re - 1.0, ab_im
    f_re = (nr * lam_re + ni * lam_im) / den
    f_im = (ni * lam_re - nr * lam_im) / den
    b_re, b_im = b_re.astype(f32), b_im.astype(f32)
    bb_re = f_re[..., None] * b_re - f_im[..., None] * b_im
    bb_im = f_re[..., None] * b_im + f_im[..., None] * b_re
    c_re, c_im = c_re.astype(f32), c_im.astype(f32)

    uf = u.astype(f32)
    nc = seq // S5_CHUNK
    ug = jnp.moveaxis(uf.reshape(bsz, nc, S5_CHUNK, S5_GROUPS, S5_GROUP), 1, 0)
    a_shape = (bsz, S5_CHUNK, S5_GROUPS, S5_STATE)
    ar = jnp.broadcast_to(ab_re, a_shape)
    ai = jnp.broadcast_to(ab_im, a_shape)

    def combine(e1, e2):
        a1r, a1i, b1r, b1i = e1
        a2r, a2i, b2r, b2i = e2
        return (a2r * a1r - a2i * a1i, a2r * a1i + a2i * a1r,
                a2r * b1r - a2i * b1i + b2r, a2r * b1i + a2i * b1r + b2i)

    def step(carry, u_blk):
        h_re, h_im = carry
        bu_re = jnp.einsum('btgh,gph->btgp', u_blk, bb_re)
        bu_im = jnp.einsum('btgh,gph->btgp', u_blk, bb_im)
        acr, aci, sr, si = lax.associative_scan(combine, (ar, ai, bu_re, bu_im), axis=1)
        sr = sr + acr * h_re[:, None] - aci * h_im[:, None]
        si = si + acr * h_im[:, None] + aci * h_re[:, None]
        y = (jnp.einsum('btgp,gop->btgo', sr, c_re)
             - jnp.einsum('btgp,gop->btgo', si, c_im))
        return (sr[:, -1], si[:, -1]), y

    zeros = jnp.zeros((bsz, S5_GROUPS, S5_STATE), f32)
    _, ys = lax.scan(step, (zeros, zeros), ug)
    y = jnp.moveaxis(ys, 0, 1).reshape(bsz, seq, D_MODEL)
    return (y + d_skip.astype(f32) * uf).astype(u.dtype)


def s5_glu_mixer(h, a_re, a_im, log_dt, b_re, b_im, c_re, c_im, d_skip, glu_w1, glu_w2):
    g = jax.nn.gelu(s5_ssm(h, a_re, a_im, log_dt, b_re, b_im, c_re, c_im, d_skip))
    return (g @ glu_w1) * jax.nn.sigmoid(g @ glu_w2)


def swiglu(h, wg, wu, wd):
    return (jax.nn.silu(h @ wg) * (h @ wu)) @ wd


def setup_inputs(seed: int = 0) -> dict:
    key = jax.random.key(seed)
    ks = jax.random.split(key, 32)
    f32 = jnp.float32

    def nrm(k, shape, s):
        return jax.random.normal(k, shape, f32) * s

    n_idx = jnp.arange(S5_STATE, dtype=f32)
    return {
        'x': nrm(ks[0], (BATCH, SEQ, D_MODEL), 1.0),
        'c': nrm(ks[1], (BATCH, D_MODEL), 1.0),
        'w_ada': nrm(ks[2], (D_MODEL, N_MOD * D_MODEL), 0.5 * D_MODEL ** -0.5),
        'b_ada': nrm(ks[3], (N_MOD * D_MODEL,), 0.02),
        'ada_table': nrm(ks[4], (DEPTH, N_MOD, D_MODEL), 0.1),
        'norm_mix': 1.0 + nrm(ks[5], (DEPTH, D_MODEL), 0.02),
        'norm_ffn': 1.0 + nrm(ks[6], (DEPTH, D_MODEL), 0.02),
        'norm_final': 1.0 + nrm(ks[7], (D_MODEL,), 0.02),
        'mix_w_in': nrm(ks[8], (N_EVEN, D_MODEL, IN_PROJ_WIDTH), D_MODEL ** -0.5),
        'conv_w': nrm(ks[9], (N_EVEN, CONV_K, CONV_CH), CONV_K ** -0.5),
        'lambda_q1': nrm(ks[10], (N_EVEN, ATTN_HEAD_DIM), 0.1),
        'lambda_k1': nrm(ks[11], (N_EVEN, ATTN_HEAD_DIM), 0.1),
        'lambda_q2': nrm(ks[12], (N_EVEN, ATTN_HEAD_DIM), 0.1),
        'lambda_k2': nrm(ks[13], (N_EVEN, ATTN_HEAD_DIM), 0.1),
        'subln_g': 1.0 + nrm(ks[14], (N_EVEN, 2 * ATTN_HEAD_DIM), 0.02),
        'mix_w_out': nrm(ks[15], (N_EVEN, MIX_WIDTH, D_MODEL), MIX_WIDTH ** -0.5),
        's5_a_re': -0.5 + nrm(ks[16], (N_ODD, S5_GROUPS, S5_STATE), 0.01),
        's5_a_im': math.pi * n_idx + nrm(ks[17], (N_ODD, S5_GROUPS, S5_STATE), 0.01),
        's5_log_dt': jax.random.uniform(ks[18], (N_ODD, S5_GROUPS), f32,
                                        math.log(DT_MIN), math.log(DT_MAX)),
        's5_b_re': nrm(ks[19], (N_ODD, S5_GROUPS, S5_STATE, S5_GROUP), (2 * S5_GROUP) ** -0.5),
        's5_b_im': nrm(ks[20], (N_ODD, S5_GROUPS, S5_STATE, S5_GROUP), (2 * S5_GROUP) ** -0.5),
        's5_c_re': nrm(ks[21], (N_ODD, S5_GROUPS, S5_GROUP, S5_STATE), S5_STATE ** -0.5),
        's5_c_im': nrm(ks[22], (N_ODD, S5_GROUPS, S5_GROUP, S5_STATE), S5_STATE ** -0.5),
        's5_d': nrm(ks[23], (N_ODD, D_MODEL), 1.0),
        'glu_w1': nrm(ks[24], (N_ODD, D_MODEL, D_MODEL), D_MODEL ** -0.5),
        'glu_w2': nrm(ks[25], (N_ODD, D_MODEL, D_MODEL), D_MODEL ** -0.5),
        'ffn_w_gate': nrm(ks[26], (DEPTH, D_MODEL, FFN_HIDDEN), D_MODEL ** -0.5),
        'ffn_w_up': nrm(ks[27], (DEPTH, D_MODEL, FFN_HIDDEN), D_MODEL ** -0.5),
        'ffn_w_down': nrm(ks[28], (DEPTH, FFN_HIDDEN, D_MODEL), FFN_HIDDEN ** -0.5),
    }


def reference(x, c, w_ada, b_ada, ada_table, norm_mix, norm_ffn, norm_final,
              mix_w_in, conv_w, lambda_q1, lambda_k1, lambda_q2, lambda_k2, subln_g, mix_w_out,
              s5_a_re, s5_a_im, s5_log_dt, s5_b_re, s5_b_im, s5_c_re, s5_c_im, s5_d,
              glu_w1, glu_w2, ffn_w_gate, ffn_w_up, ffn_w_down):
    dtype = x.dtype
    mod = (jax.nn.silu(c) @ w_ada + b_ada).reshape(c.shape[0], N_MOD, D_MODEL)
    for l in range(DEPTH):
        m = mod + ada_table[l][None]
        sh_m, sc_m, g_m, sh_f, sc_f, g_f = (m[:, 0], m[:, 1], m[:, 2],
                                            m[:, 3], m[:, 4], m[:, 5])
        h = modulate(rmsnorm(x, norm_mix[l]), sh_m, sc_m)
        if l % 2 == 0:
            e = l // 2
            lam_init = 0.8 - 0.6 * math.exp(-0.3 * l)
            y = conv_diffattn_mixer(h, mix_w_in[e], conv_w[e], lambda_q1[e], lambda_k1[e],
                                    lambda_q2[e], lambda_k2[e], subln_g[e], mix_w_out[e],
                                    lam_init)
        else:
            o = l // 2
            y = s5_glu_mixer(h, s5_a_re[o], s5_a_im[o], s5_log_dt[o], s5_b_re[o], s5_b_im[o],
                             s5_c_re[o], s5_c_im[o], s5_d[o], glu_w1[o], glu_w2[o])
        x = x + (g_m[:, None, :] * y).astype(dtype)
        h = modulate(rmsnorm(x, norm_ffn[l]), sh_f, sc_f)
        f = swiglu(h, ffn_w_gate[l], ffn_w_up[l], ffn_w_down[l])
        x = x + (g_f[:, None, :] * f).astype(dtype)
    return rmsnorm(x, norm_final)
```

```python
import math
import contextlib
import numpy as np
import ml_dtypes
import concourse.bass as bass
import concourse.mybir as mybir
from concourse.bass_utils import run_bass_kernel_spmd

F32 = mybir.dt.float32
BF16 = mybir.dt.bfloat16
ACT = mybir.ActivationFunctionType
ALU = mybir.AluOpType
AX = mybir.AxisListType

D = 4096
NCH = 32
NCORE = 8
TN = 512
FFN_H = 11008
HCH = 11
HPAD = HCH * 128 * NCORE
NORM_EPS = 1e-6
SUBLN_EPS = 1e-5
S5L = 512


class Buf:
    __slots__ = ("w", "r")

    def __init__(self):
        self.w = None
        self.r = {}


class Stream:
    def __init__(self, name):
        self.name = name
        self.ops = []
        self.waited = {}
        self.cnt = 0
        self.sem = None
        self.ring = []
        self.ring_i = 0


class Tracker:
    def __init__(self):
        self.streams = {n: Stream(n) for n in ("pe", "act", "dve", "pool", "sp")}
        self.nsem = 0
        for n in ("pe", "act", "dve", "pool"):
            self.streams[n].sem = self._new()
        for n, k in (("sp", 8), ("pool", 8), ("act", 4)):
            self.streams[n].ring = [self._new() for _ in range(k)]
        self.cc_ring = [self._new() for _ in range(4)]
        self.cc_i = 0
        self.latest = {}

    def _new(self):
        self.nsem += 1
        return self.nsem - 1

    def wait(self, s, tok):
        if tok is None:
            return
        k, v = tok
        if s.waited.get(k, 0) >= v:
            return
        s.waited[k] = v
        s.ops.append(("w", k, v))

    def _deps(self, s, reads, writes, acc):
        for b in reads:
            self.wait(s, b.w)
        for b in writes:
            if not (acc and b.w is not None and b.w[0] == s.sem):
                self.wait(s, b.w)
            for k, v in b.r.items():
                self.wait(s, (k, v))

    def _mark(self, tok, reads, writes):
        k, v = tok
        self.latest[k] = v
        for b in reads:
            if b.r.get(k, 0) < v:
                b.r[k] = v
        for b in writes:
            b.w = tok
            b.r = {}

    def op(self, sname, fn, reads=(), writes=(), acc=False):
        s = self.streams[sname]
        self._deps(s, reads, writes, acc)
        s.cnt += 1
        tok = (s.sem, s.cnt)
        s.ops.append(("i", fn, s.sem, 1))
        self._mark(tok, reads, writes)
        return tok

    def dma(self, sname, fn, reads=(), writes=()):
        s = self.streams[sname]
        self._deps(s, reads, writes, False)
        i = s.ring_i
        s.ring_i += 1
        K = len(s.ring)
        k = s.ring[i % K]
        use = i // K
        if use > 0:
            self.wait(s, (k, 16 * use))
        tok = (k, 16 * (use + 1))
        s.ops.append(("i", fn, k, 16))
        self._mark(tok, reads, writes)
        return tok

    def coll(self, fn, reads=(), writes=()):
        s = self.streams["pool"]
        self._deps(s, reads, writes, False)
        i = self.cc_i
        self.cc_i += 1
        K = len(self.cc_ring)
        k = self.cc_ring[i % K]
        use = i // K
        if use > 0:
            self.wait(s, (k, use))
        tok = (k, use + 1)
        s.ops.append(("i", fn, k, 1))
        self._mark(tok, reads, writes)
        return tok

    def barrier(self):
        toks = list(self.latest.items())
        for s in self.streams.values():
            for tok in toks:
                self.wait(s, tok)


class Prog:
    def __init__(self, SEQ, DEPTH, do_mix=True, dbg=None, l0=0, do_final=True):
        self.l0 = l0
        self.do_final = do_final
        self.SEQ = SEQ
        self.DEPTH = DEPTH
        self.do_mix = do_mix
        self.dbg = dbg
        self.NT = SEQ // TN
        self.TPS = self.NT // NCORE
        assert self.TPS >= 1
        self.nc = bass.Bass("TRN2", target_bir_lowering=False)
        self.tr = Tracker()
        self.stack = contextlib.ExitStack()
        self.dram = {}
        self.in_names = []
        self.uid = 0

    def slot(self, t):
        return (t % self.TPS) * NCORE + t // self.TPS

    def fused_src(self, nb, t, pend, ybufs):
        def src(j):
            xt, bxt = nb["xt"], nb["bxt"]
            self.load(xt[:, j, :], self.xa_ap(t, j), [self.bXA[t][j]], [bxt[j]])
            yb, byb = ybufs[0][j % 2], ybufs[1][j % 2]
            if pend[0] == "inc":
                r0 = (self.slot(t) * NCH + j) * 128
                self.load(yb[:], self.INC[r0:r0 + 128, :], [self.bINC], [byb])
                self.tt("dve", xt[:, j, :], xt[:, j, :], yb[:], ALU.add, [byb, bxt[j]], [bxt[j]])
            else:
                r, kk = divmod(j, 4)
                r0 = self.gall_row(r, t, kk)
                self.load(yb[:], self.YALL[r0:r0 + 128, :], [pend[1]], [byb])
                self.stt(xt[:, j, :], yb[:], pend[2][:, j:j + 1], xt[:, j, :], ALU.mult, ALU.add,
                         [byb, self.bVEC, bxt[j]], [bxt[j]])
            self.store(self.xa_ap(t, j), xt[:, j, :], [bxt[j]], [self.bXA[t][j]])
        return src

    def take_pending(self, ph):
        pend = self.pending
        self.pending = None
        yb = [self.sb(ph, f"ybuf{i}", [128, TN], BF16) for i in range(2)]
        return pend, (yb, [Buf(), Buf()])

    def alloc_ep(self, ph):
        return {"i": 0,
                "x8": [self.sb(ph, f"x8{i}", [128, TN]) for i in range(2)], "bx8": [Buf(), Buf()],
                "po": [self.sb(ph, f"po{i}", [128, TN]) for i in range(2)], "bpo": [Buf(), Buf()],
                "pob": [self.sb(ph, f"pob{i}", [128, TN], BF16) for i in range(2)], "bpob": [Buf(), Buf()]}

    def din(self, name, shape, dtype=F32):
        t = self.nc.dram_tensor(name, list(shape), dtype, kind="ExternalInput").ap()
        self.in_names.append(name)
        return t

    def dout(self, name, shape, dtype=F32):
        return self.nc.dram_tensor(name, list(shape), dtype, kind="ExternalOutput").ap()

    def dtmp(self, name, shape, dtype=F32):
        return self.nc.dram_tensor(name, list(shape), dtype).ap()

    def sb(self, stack, name, shape, dtype=F32):
        self.uid += 1
        return stack.enter_context(self.nc.sbuf_tensor(f"{name}_{self.uid}", list(shape), dtype))

    def ps(self, stack, name, shape, dtype=F32):
        self.uid += 1
        return stack.enter_context(self.nc.psum_tensor(f"{name}_{self.uid}", list(shape), dtype))

    def mm(self, out, lhsT, rhs, start, stop, reads, writes):
        self.tr.op("pe", lambda e: e.matmul(out, lhsT, rhs, start=start, stop=stop),
                   reads, writes, acc=not start)

    def transpose(self, out, in_, ident, reads, writes, acc=False):
        self.tr.op("pe", lambda e: e.transpose(out, in_, ident), reads, writes, acc=acc)

    def reduce(self, out, in_, op, reads, writes):
        self.tr.op("dve", lambda e: e.tensor_reduce(out, in_, AX.X, op), reads, writes)

    def recip(self, out, in_, reads, writes):
        self.tr.op("dve", lambda e: e.reciprocal(out, in_), reads, writes)

    def act(self, out, in_, func, reads, writes, bias=None, scale=None, accum_out=None):
        kw = {}
        if bias is not None:
            kw["bias"] = bias
        if scale is not None:
            kw["scale"] = scale
        if accum_out is not None:
            kw["accum_out"] = accum_out
        self.tr.op("act", lambda e: e.activation(out, in_, func, **kw), reads, writes)

    def tt(self, eng, out, in0, in1, op, reads, writes):
        self.tr.op(eng, lambda e: e.tensor_tensor(out, in0, in1, op), reads, writes)

    def ts(self, eng, out, in0, s1, s2, op0, op1, reads, writes):
        if op1 is None:
            self.tr.op(eng, lambda e: e.tensor_scalar(out, in0, s1, None, op0), reads, writes)
        else:
            self.tr.op(eng, lambda e: e.tensor_scalar(out, in0, s1, s2, op0, op1), reads, writes)

    def stt(self, out, in0, scalar, in1, op0, op1, reads, writes):
        self.tr.op("dve", lambda e: e.scalar_tensor_tensor(out, in0, scalar, in1, op0, op1),
                   reads, writes)

    def copy(self, eng, out, in_, reads, writes):
        if eng == "act":
            self.tr.op("act", lambda e: e.activation(out, in_, ACT.Identity), reads, writes)
        else:
            self.tr.op(eng, lambda e: e.tensor_copy(out, in_), reads, writes)

    def memset(self, eng, ap, val, writes):
        self.tr.op(eng, lambda e: e.memset(ap, val), (), writes)

    def load(self, out, in_, reads, writes, q="sp"):
        self.tr.dma(q, lambda e: e.dma_start(out=out, in_=in_), reads, writes)

    def store(self, out, in_, reads, writes, q="pool"):
        self.tr.dma(q, lambda e: e.dma_start(out=out, in_=in_), reads, writes)

    def collective(self, kind, op, in_ap, out_ap, reads, writes):
        rg = [list(range(NCORE))]
        self.tr.coll(lambda e: e.collective_compute(kind, op, rg, ins=[in_ap.opt()], outs=[out_ap.opt()]),
                     reads, writes)

    def build(self):
        nc = self.nc
        NT, TPS, DEPTH = self.NT, self.TPS, self.DEPTH
        st = self.stack
        x_in = self.din("x_in", [TPS * NCH * 128, TN])
        c_in = self.din("c_in", [128, NCH])
        wada = self.din("wada", [24, 128, NCH * 128])
        bada = self.din("bada", [128, 24])
        adat = self.din("adat", [128, DEPTH * 6 * NCH])
        nmix = self.din("nmix", [128, DEPTH * NCH])
        nffn = self.din("nffn", [128, DEPTH * NCH])
        nfin = self.din("nfin", [128, NCH])
        wg_in = [self.din(f"wg{l}", [HCH * 128, D]) for l in range(DEPTH)]
        wu_in = [self.din(f"wu{l}", [HCH * 128, D]) for l in range(DEPTH)]
        wd_in = [self.din(f"wd{l}", [NCH * 128, HCH * 128]) for l in range(DEPTH)]
        out = self.dout("out", [TPS * NCH * 128, TN])
        self.out_ap = out
        NE = (DEPTH + 1) // 2 if self.do_mix else 0
        EW = []
        if NE:
            self.masks_in = self.din("masks", [128, 4 * TN])
            self.ident_in = self.din("ident", [128, 128])
        for e in range(NE):
            EW.append({
                "wi": self.din(f"wi{e}", [10 * 128, D]), "wv": self.din(f"wv{e}", [128, NCH * 256]),
                "wo": self.din(f"wo{e}", [128, NCH * 4 * 128]), "cw": self.din(f"cw{e}", [128, 6]),
                "lam": self.din(f"lam{e}", [128, 512]), "sgr": self.din(f"sgr{e}", [128, 256]),
                "wi_b": self.dtmp(f"wib{e}", [10 * 128, D], BF16), "wv_b": self.dtmp(f"wvb{e}", [128, NCH * 256], BF16),
                "wo_b": self.dtmp(f"wob{e}", [128, NCH * 4 * 128], BF16),
            })
        NO = DEPTH // 2 if self.do_mix else 0
        OW = []
        if NO:
            self.iota_in = self.din("iota", [128, S5L])
            sel_in = self.din("sel", [4 * 128, D])
            self.sel_b = self.dtmp("selb", [4 * 128, D], BF16)
            self.UC = self.dtmp("UC", [4 * NT * 128, TN], BF16)
            self.GC = self.dtmp("GC", [NT * 4 * 128, TN], BF16)
            self.GALL = self.dtmp("GALL", [NCORE * NT * 4 * 128, TN], BF16)
            self.YC = self.dtmp("YC", [NT * 4 * 128, TN], BF16)
            self.YALL = self.dtmp("YALL", [NCORE * NT * 4 * 128, TN], BF16)
        for o_ in range(NO):
            OW.append({
                "are": self.din(f"are{o_}", [128, 16]), "aim": self.din(f"aim{o_}", [128, 16]),
                "ldt": self.din(f"ldt{o_}", [128, 16]), "dsk": self.din(f"dsk{o_}", [128, 4]),
                "bre": self.din(f"bre{o_}", [128, 2048]), "bim": self.din(f"bim{o_}", [128, 2048]),
                "cre": self.din(f"cre{o_}", [128, 2048]), "cim": self.din(f"cim{o_}", [128, 2048]),
                "w1": self.din(f"gw1{o_}", [4 * 128, D]), "w2": self.din(f"gw2{o_}", [4 * 128, D]),
                "bre_b": self.dtmp(f"breb{o_}", [128, 2048], BF16), "bim_b": self.dtmp(f"bimb{o_}", [128, 2048], BF16),
                "w1_b": self.dtmp(f"gw1b{o_}", [4 * 128, D], BF16), "w2_b": self.dtmp(f"gw2b{o_}", [4 * 128, D], BF16),
            })
        if NE:
            self.YM = self.dtmp("YM", [NT * 4 * 128, TN], BF16)
            self.YA = self.dtmp("YA", [NT * 2 * 128, TN], BF16)
            self.QT = self.dtmp("QT", [2 * NT * 128, TN], BF16)
            self.KT = self.dtmp("KT", [2 * NT * 128, TN], BF16)
            self.VV = self.dtmp("VV", [NT * 4 * 128, 256], BF16)

        XSI = self.dtmp("XSI", [TPS * NCH * 128, TN])
        XA = self.dtmp("XA", [NT * NCH * 128, TN])
        PB = self.dtmp("PB", [NT * NCH * 128, TN])
        XF = self.dtmp("XF", [TPS * NCH * 128, TN])
        self.RSB = self.dtmp("RSB", [TPS * NCH * 128, TN])
        MODL = self.dtmp("MODL", [128, 24])
        MODA = self.dtmp("MODA", [NCORE * 128, 24])
        wg_b = [self.dtmp(f"wgb{l}", [HCH * 128, D], BF16) for l in range(DEPTH)]
        wu_b = [self.dtmp(f"wub{l}", [HCH * 128, D], BF16) for l in range(DEPTH)]
        wd_b = [self.dtmp(f"wdb{l}", [NCH * 128, HCH * 128], BF16) for l in range(DEPTH)]

        self.bXA = [[Buf() for _ in range(NCH)] for _ in range(NT)]
        self.bPB = [[Buf() for _ in range(NCH)] for _ in range(NT)]
        bXSI = Buf()
        bMODL, bMODA = Buf(), Buf()
        bXF = Buf()
        self.XA, self.PB = XA, PB

        slot = self.slot
        self.PBH = self.dtmp("PBH", [NT * NCH * 128, TN], BF16)
        self.RSH = self.dtmp("RSH", [NT * NCH * 128 // NCORE, TN], BF16)
        self.INC = self.dtmp("INC", [NT * NCH * 128, TN], BF16)
        self.bINC = Buf()
        self.pending = None

        def xa_ap(t, j):
            r0 = (slot(t) * NCH + j) * 128
            return XA[r0:r0 + 128, :]

        def pb_ap(t, j):
            r0 = (slot(t) * NCH + j) * 128
            return PB[r0:r0 + 128, :]
        self.xa_ap, self.pb_ap = xa_ap, pb_ap

        consts = self.sb(st, "consts", [128, 8])
        bconsts = Buf()
        ones_bf = self.sb(st, "ones_bf", [128, 128], BF16)
        b_ones = Buf()
        VEC = self.sb(st, "VEC", [128, DEPTH * 6 * NCH])
        bVEC = Buf()
        AFIN = self.sb(st, "AFIN", [128, NCH])
        bAFIN = Buf()
        self.VEC, self.bVEC = VEC, bVEC
        self.consts, self.bconsts = consts, bconsts
        self.ones_bf, self.b_ones = ones_bf, b_ones
        self.ZSH = self.sb(st, "ZSH", [128, NCH]); self.bZSH = Buf()
        self.memset("dve", self.ZSH[:], 0.0, [self.bZSH])
        self.memset("dve", consts[:, 0:1], NORM_EPS, [bconsts])
        self.memset("dve", consts[:, 1:2], 0.0, [bconsts])
        self.memset("dve", consts[:, 2:3], SUBLN_EPS, [bconsts])
        self.memset("dve", consts[:, 3:4], 1.0, [bconsts])
        self.memset("dve", ones_bf[:], 1.0, [b_ones])

        self.load(XSI[:], x_in[:], [], [bXSI], q="pool")
        CR = NCH * 128
        for q in range(TPS):
            wb = [b for r in range(NCORE) for b in self.bXA[r * TPS + q]]
            self.collective("AllGather", ALU.bypass, XSI[q * CR:(q + 1) * CR, :],
                            XA[q * NCORE * CR:(q + 1) * NCORE * CR, :], [bXSI], wb)

        self.prologue_ada(c_in, wada, bada, adat, nmix, nffn, nfin, MODL, MODA, bMODL, bMODA, AFIN, bAFIN)

        for l in range(DEPTH):
            self.cast_weight(wg_in[l], wg_b[l], HCH, D)
            self.cast_weight(wu_in[l], wu_b[l], HCH, D)
            self.cast_weight(wd_in[l], wd_b[l], NCH, HCH * 128)
        for e in range(NE):
            self.cast_weight(EW[e]["wi"], EW[e]["wi_b"], 10, D)
            self.cast_weight(EW[e]["wv"], EW[e]["wv_b"], 1, NCH * 256)
            self.cast_weight(EW[e]["wo"], EW[e]["wo_b"], 1, NCH * 4 * 128)
        if NO:
            self.cast_weight(sel_in, self.sel_b, 4, D)
        for o_ in range(NO):
            self.cast_weight(OW[o_]["bre"], OW[o_]["bre_b"], 1, 2048)
            self.cast_weight(OW[o_]["bim"], OW[o_]["bim_b"], 1, 2048)
            self.cast_weight(OW[o_]["w1"], OW[o_]["w1_b"], 4, D)
            self.cast_weight(OW[o_]["w2"], OW[o_]["w2_b"], 4, D)
        self.tr.barrier()

        for l in range(DEPTH):
            last = (l == DEPTH - 1)
            bYALL = None
            if self.do_mix and l % 2 == 0:
                if self.dbg != "oddonly":
                    self.even_mixer(l, EW[l // 2])
            elif self.do_mix:
                bYALL = self.odd_mixer(l, OW[l // 2])
            if self.dbg == "STOP":
                self.emit()
                return nc
            self.ffn_layer(l, wg_b[l], wu_b[l], wd_b[l], last, XF, bXF, bYALL)
            self.tr.barrier()

        self.final_norm(XF, bXF, out, AFIN, bAFIN)
        self.tr.barrier()
        self.emit()
        return nc

    def prologue_ada(self, c_in, wada, bada, adat, nmix, nffn, nfin, MODL, MODA, bMODL, bMODA, AFIN, bAFIN):
        DEPTH = self.DEPTH
        with contextlib.ExitStack() as ph:
            csb = self.sb(ph, "csb", [128, NCH]); bc = Buf()
            sc = self.sb(ph, "sc", [128, NCH]); bsc = Buf()
            wab = [self.sb(ph, f"wab{i}", [128, NCH * 128]) for i in range(2)]
            bwab = [Buf(), Buf()]
            bad = self.sb(ph, "bad", [128, 24]); bbad = Buf()
            modl = self.sb(ph, "modl", [128, 24]); bmodl = Buf()
            MOD = self.sb(ph, "MOD", [128, NCORE * 24]); bMOD = Buf()
            adt = self.sb(ph, "adt", [128, DEPTH * 6 * NCH]); badt = Buf()
            nm = self.sb(ph, "nm", [128, DEPTH * NCH]); bnm = Buf()
            nf = self.sb(ph, "nf", [128, DEPTH * NCH]); bnf = Buf()
            self.load(csb[:], c_in[:], [], [bc])
            self.load(bad[:], bada[:], [], [bbad])
            self.load(adt[:], adat[:], [], [badt])
            self.load(nm[:], nmix[:], [], [bnm])
            self.load(nf[:], nffn[:], [], [bnf])
            self.load(self_ap(AFIN), nfin[:], [], [bAFIN])
            self.act(sc[:], csb[:], ACT.Silu, [bc], [bsc])
            ps = self.ps(ph, "psada", [128, TN]); bps = Buf()
            for q in range(24):
                w = wab[q % 2]; bw = bwab[q % 2]
                self.load(w[:], wada[q], [], [bw])
                for j in range(NCH):
                    self.mm(ps[:, q:q + 1], w[:, j * 128:(j + 1) * 128], sc[:, j:j + 1],
                            j == 0, j == NCH - 1, [bw, bsc], [bps])
            self.tt("dve", modl[:], ps[:, 0:24], bad[:], ALU.add, [bps, bbad], [bmodl])
            self.store(MODL[:], modl[:], [bmodl], [bMODL])
            self.collective("AllGather", ALU.bypass, MODL, MODA, [bMODL], [bMODA])
            self.load(MOD[:].rearrange("p (r q) -> p r q", r=NCORE),
                      MODA.rearrange("(r p) q -> p r q", p=128), [bMODA], [bMOD])
            VEC, bVEC = self.VEC, self.bVEC
            for l in range(DEPTH):
                o = l * 6 * NCH
                self.tt("dve", adt[:, o:o + 6 * NCH], adt[:, o:o + 6 * NCH], MOD[:], ALU.add,
                        [bMOD, badt], [badt])
                for half, nrm in ((0, nm), (1, nf)):
                    k0 = o + half * 3 * NCH
                    self.stt(VEC[:, k0:k0 + NCH], adt[:, k0 + NCH:k0 + 2 * NCH], 1.0,
                             nrm[:, l * NCH:(l + 1) * NCH], ALU.add, ALU.mult,
                             [badt, bnm, bnf], [bVEC])
                    self.copy("dve", VEC[:, k0 + NCH:k0 + 2 * NCH], adt[:, k0:k0 + NCH], [badt], [bVEC])
                    self.copy("dve", VEC[:, k0 + 2 * NCH:k0 + 3 * NCH], adt[:, k0 + 2 * NCH:k0 + 3 * NCH],
                              [badt], [bVEC])
            self.tr.barrier()

    def cast_weight(self, src, dst, nblk, cols):
        with contextlib.ExitStack() as ph:
            CW = min(cols, 4096)
            f = [self.sb(ph, f"cf{i}", [128, CW]) for i in range(2)]
            g = [self.sb(ph, f"cg{i}", [128, CW], BF16) for i in range(2)]
            bf = [Buf(), Buf()]; bg = [Buf(), Buf()]
            bd = Buf()
            it = 0
            for b in range(nblk):
                for c0 in range(0, cols, CW):
                    cw = min(CW, cols - c0)
                    i = it % 2
                    self.load(f[i][:, 0:cw], src[b * 128:(b + 1) * 128, c0:c0 + cw], [], [bf[i]])
                    eng = ("pool", "dve", "act")[it % 3]
                    self.copy(eng, g[i][:, 0:cw], f[i][:, 0:cw], [bf[i]], [bg[i]])
                    self.store(dst[b * 128:(b + 1) * 128, c0:c0 + cw], g[i][:, 0:cw], [bg[i]], [bd], q="act")
                    it += 1
            self.tr.barrier()

    def norm_tile(self, ph_bufs, t, A_ap, sh_ap, bvec, out_dtype_bf16=True, src=None):
        xt, bxt = ph_bufs["xt"], ph_bufs["bxt"]
        hT, bhT = ph_bufs["hT"], ph_bufs["bhT"]
        sq, bsq = ph_bufs["sq"], ph_bufs["bsq"]
        tmp, btmp = ph_bufs["tmp"], ph_bufs["btmp"]
        rstd, brstd = ph_bufs["rstd"], ph_bufs["brstd"]
        pst, bpst = ph_bufs["pst"], ph_bufs["bpst"]
        for j in range(NCH):
            if src is None:
                self.load(xt[:, j, :], self.xa_ap(t, j), [self.bXA[t][j]], [bxt[j]])
            else:
                src(j)
            i = j % 2
            self.act(sq[i][:], xt[:, j, :], ACT.Square, [bxt[j]], [bsq[i]])
            self.mm(pst[:], self.ones_bf[:], sq[i][:], j == 0, j == NCH - 1, [self.b_ones, bsq[i]], [bpst])
        self.act(tmp[0][:], pst[:], ACT.Sqrt, [bpst, self.bconsts], [btmp[0]],
                 bias=self.consts[:, 0:1], scale=1.0 / D)
        self.tr.op("dve", lambda e: e.reciprocal(rstd[:], tmp[0][:]), [btmp[0]], [brstd])
        for j in range(NCH):
            i = j % 2
            self.stt(tmp[i][:], xt[:, j, :], A_ap[:, j:j + 1], rstd[:], ALU.mult, ALU.mult,
                     [bxt[j], brstd] + bvec, [btmp[i]])
            self.act(hT[:, j, :], tmp[i][:], ACT.Identity, [btmp[i]] + bvec, [bhT[j]],
                     bias=sh_ap[:, j:j + 1], scale=1.0)

    def alloc_norm_bufs(self, ph, h_dtype=BF16):
        d = {}
        d["xt"] = self.sb(ph, "xt", [128, NCH, TN]); d["bxt"] = [Buf() for _ in range(NCH)]
        d["hT"] = self.sb(ph, "hT", [128, NCH, TN], h_dtype); d["bhT"] = [Buf() for _ in range(NCH)]
        d["sq"] = [self.sb(ph, f"sq{i}", [128, TN], BF16) for i in range(2)]; d["bsq"] = [Buf(), Buf()]
        d["tmp"] = [self.sb(ph, f"tmp{i}", [128, TN]) for i in range(2)]; d["btmp"] = [Buf(), Buf()]
        d["rstd"] = self.sb(ph, "rstd", [128, TN]); d["brstd"] = Buf()
        d["pst"] = self.ps(ph, "pst", [128, TN]); d["bpst"] = Buf()
        return d

    def epilogue(self, nb, t, n, ps, bps, g_ap, bvec, ep, last=False):
        i = ep["i"]; ep["i"] += 1
        if not last:
            pob, bpob = ep["pob"][i % 2], ep["bpob"][i % 2]
            self.ts("dve", pob[:], ps[:], g_ap[:, n:n + 1], None, ALU.mult, None, [bps, bvec], [bpob])
            r0 = (self.slot(t) * NCH + n) * 128
            self.store(self.PBH[r0:r0 + 128, :], pob[:], [bpob], [self.bPB[t][n]])
            return
        x8, bx8 = ep["x8"][i % 2], ep["bx8"][i % 2]
        po, bpo = ep["po"][i % 2], ep["bpo"][i % 2]
        self.act(x8[:], nb["xt"][:, n, :], ACT.Identity, [nb["bxt"][n]], [bx8], scale=0.125)
        self.stt(po[:], ps[:], g_ap[:, n:n + 1], x8[:], ALU.mult, ALU.add, [bps, bvec, bx8], [bpo])
        self.store(self.pb_ap(t, n), po[:], [bpo], [self.bPB[t][n]])

    def reduce_all(self, last, XF, bXF):
        TPS = self.TPS
        CR = NCH * 128
        for q in range(TPS):
            rb = [b for r in range(NCORE) for b in self.bPB[r * TPS + q]]
            src = self.PB[q * NCORE * CR:(q + 1) * NCORE * CR, :]
            if last:
                self.collective("ReduceScatter", ALU.add, src, XF[q * CR:(q + 1) * CR, :], rb, [bXF])
        if last:
            return
        NT = self.NT
        TCH = min(16, NT)
        CRh = TCH * NCH * 128
        allpb = [b for row in self.bPB for b in row]
        for h in range(NT // TCH):
            brs = Buf()
            rsh = self.RSH[h * CRh // NCORE:(h + 1) * CRh // NCORE, :]
            self.collective("ReduceScatter", ALU.add, self.PBH[h * CRh:(h + 1) * CRh, :], rsh, allpb, [brs])
            self.collective("AllGather", ALU.bypass, rsh, self.INC[h * CRh:(h + 1) * CRh, :], [brs], [self.bINC])
        self.pending = ("inc",)

    def ffn_layer(self, l, wg, wu, wd, last, XF, bXF, bYALL=None):
        NT, TPS = self.NT, self.TPS
        VEC, bVEC = self.VEC, self.bVEC
        o = l * 6 * NCH + 3 * NCH
        A_ap = VEC[:, o:o + NCH]; sh_ap = VEC[:, o + NCH:o + 2 * NCH]; g_ap = VEC[:, o + 2 * NCH:o + 3 * NCH]
        with contextlib.ExitStack() as ph:
            nb = self.alloc_norm_bufs(ph)
            aT = self.sb(ph, "aT", [128, HCH, TN], BF16); baT = [Buf() for _ in range(HCH)]
            wgs = [self.sb(ph, f"wgs{i}", [128, D], BF16) for i in range(2)]; bwg = [Buf(), Buf()]
            wus = [self.sb(ph, f"wus{i}", [128, D], BF16) for i in range(2)]; bwu = [Buf(), Buf()]
            wds = [self.sb(ph, f"wds{i}", [128, HCH * 128], BF16) for i in range(3)]; bwd = [Buf() for _ in range(3)]
            sg = [self.sb(ph, f"sg{i}", [128, TN]) for i in range(2)]; bsg = [Buf(), Buf()]
            ep = self.alloc_ep(ph)
            hT, bhT = nb["hT"], nb["bhT"]
            self.PS = [self.ps(ph, f"ps{i}", [128, TN]) for i in range(6)]
            self.bPS = [Buf() for _ in range(6)]
            it = 0
            pend, ybufs = self.take_pending(ph)
            for t in range(NT):
                src = self.fused_src(nb, t, pend, ybufs) if pend else None
                self.norm_tile(nb, t, A_ap, sh_ap, [bVEC], src=src)
                for m in range(HCH):
                    i = it % 2; it += 1
                    self.load(wgs[i][:], wg[m * 128:(m + 1) * 128, :], [], [bwg[i]])
                    self.load(wus[i][:], wu[m * 128:(m + 1) * 128, :], [], [bwu[i]])
                    pg, bpg = self.PS[i], self.bPS[i]
                    pu, bpu = self.PS[2 + i], self.bPS[2 + i]
                    for j in range(NCH):
                        self.mm(pg[:], wgs[i][:, j * 128:(j + 1) * 128], hT[:, j, :], j == 0, j == NCH - 1,
                                [bwg[i], bhT[j]], [bpg])
                    for j in range(NCH):
                        self.mm(pu[:], wus[i][:, j * 128:(j + 1) * 128], hT[:, j, :], j == 0, j == NCH - 1,
                                [bwu[i], bhT[j]], [bpu])
                    self.act(sg[i][:], pg[:], ACT.Silu, [bpg], [bsg[i]])
                    self.tt("dve", aT[:, m, :], sg[i][:], pu[:], ALU.mult, [bsg[i], bpu], [baT[m]])
                for n in range(NCH):
                    i3 = n % 3
                    self.load(wds[i3][:], wd[n * 128:(n + 1) * 128, :], [], [bwd[i3]])
                    pd, bpd = self.PS[4 + n % 2], self.bPS[4 + n % 2]
                    for m in range(HCH):
                        self.mm(pd[:], wds[i3][:, m * 128:(m + 1) * 128], aT[:, m, :], m == 0, m == HCH - 1,
                                [bwd[i3], baT[m]], [bpd])
                    self.epilogue(nb, t, n, pd, bpd, g_ap, bVEC, ep, last)
            self.reduce_all(last, XF, bXF)


    def even_mixer(self, l, W):
        NT, SEQ, TPS = self.NT, self.SEQ, self.TPS
        lam_init = 0.8 - 0.6 * math.exp(-0.3 * (l + self.l0))
        VEC, bVEC = self.VEC, self.bVEC
        o = l * 6 * NCH
        A_ap = VEC[:, o:o + NCH]; sh_ap = VEC[:, o + NCH:o + 2 * NCH]; g_ap = VEC[:, o + 2 * NCH:o + 3 * NCH]
        YM, QT, KT, VV = self.YM, self.QT, self.KT, self.VV
        bYM = [[Buf() for _ in range(4)] for _ in range(NT)]
        bQT = [Buf() for _ in range(NT)]
        bKT = [Buf() for _ in range(NT)]
        bVV = [Buf() for _ in range(NT)]

        YA = self.YA

        def ym_ap(t, k):
            if k >= 2:
                r0 = (t * 2 + k - 2) * 128
                return YA[r0:r0 + 128, :]
            r0 = (t * 4 + k) * 128
            return YM[r0:r0 + 128, :]

        def qk_ap(T, mp, t):
            r0 = (mp * NT + t) * 128
            return T[r0:r0 + 128, :]

        ATTN_SCALE = 128 ** -0.5
        with contextlib.ExitStack() as ph:
            nb = self.alloc_norm_bufs(ph)
            hT, bhT = nb["hT"], nb["bhT"]
            PS = [self.ps(ph, f"ps{i}", [128, TN]) for i in range(6)]
            bPS = [Buf() for _ in range(6)]
            wis = [self.sb(ph, f"wis{i}", [128, D], BF16) for i in range(3)]
            bwi = [Buf() for _ in range(3)]
            wvs = self.sb(ph, "wvs", [128, NCH * 256], BF16); bwv = Buf()
            cws = self.sb(ph, "cws", [128, 6]); bcw = Buf()
            U = [self.sb(ph, f"U{c}", [128, TN + 2]) for c in range(2)]; bU = [Buf(), Buf()]
            gbs = [self.sb(ph, f"gbs{c}", [128, TN]) for c in range(2)]; bgb = [Buf(), Buf()]
            gcs = [self.sb(ph, f"gcs{c}", [128, TN]) for c in range(2)]; bgc = [Buf(), Buf()]
            t1 = self.sb(ph, "t1", [128, TN]); bt1 = Buf()
            t2 = self.sb(ph, "t2", [128, TN]); bt2 = Buf()
            yc = [self.sb(ph, f"yc{c}", [128, TN], BF16) for c in range(2)]; byc = [Buf(), Buf()]
            qk = [self.sb(ph, f"qk{i}", [128, TN], BF16) for i in range(2)]; bqk = [Buf(), Buf()]
            vs = [self.sb(ph, f"vs{i}", [128, 256], BF16) for i in range(2)]; bvs = [Buf(), Buf()]
            self.load(wvs[:], W["wv_b"][:], [], [bwv])
            self.load(cws[:], W["cw"][:], [], [bcw])
            for c in range(2):
                self.memset("dve", U[c][:, 0:2], 0.0, [bU[c]])
            it = 0
            psi = 0
            pend, ybufs = self.take_pending(ph)
            for t in range(NT):
                self.norm_tile(nb, t, A_ap, sh_ap, [bVEC], src=self.fused_src(nb, t, pend, ybufs) if pend else None)
                for blk in range(10):
                    i = it % 3; it += 1
                    self.load(wis[i][:], W["wi_b"][blk * 128:(blk + 1) * 128, :], [], [bwi[i]])
                    p, bp = PS[psi % 6], bPS[psi % 6]; psi += 1
                    for j in range(NCH):
                        self.mm(p[:], wis[i][:, j * 128:(j + 1) * 128], hT[:, j, :], j == 0, j == NCH - 1,
                                [bwi[i], bhT[j]], [bp])
                    if blk < 6:
                        c, kind = divmod(blk, 3)
                        if kind == 0:
                            self.copy("act", gbs[c][:], p[:], [bp], [bgb[c]])
                        elif kind == 1:
                            self.copy("act", gcs[c][:], p[:], [bp], [bgc[c]])
                        else:
                            if t > 0:
                                self.copy("dve", U[c][:, 0:2], U[c][:, TN:TN + 2], [bU[c]], [bU[c]])
                            self.tt("dve", U[c][:, 2:TN + 2], gcs[c][:], p[:], ALU.mult, [bgc[c], bp, bU[c]], [bU[c]])
                            self.ts("pool", t1[:], U[c][:, 0:TN], cws[:, c * 3:c * 3 + 1], None, ALU.mult, None,
                                    [bU[c], bcw], [bt1])
                            self.stt(t2[:], U[c][:, 1:TN + 1], cws[:, c * 3 + 1:c * 3 + 2], t1[:], ALU.mult, ALU.add,
                                     [bU[c], bcw, bt1], [bt2])
                            self.stt(t1[:], U[c][:, 2:TN + 2], cws[:, c * 3 + 2:c * 3 + 3], t2[:], ALU.mult, ALU.add,
                                     [bU[c], bcw, bt2], [bt1])
                            self.tt("dve", yc[c][:], t1[:], gbs[c][:], ALU.mult, [bt1, bgb[c]], [byc[c]])
                            self.store(ym_ap(t, c), yc[c][:], [byc[c]], [bYM[t][c]])
                    else:
                        qi = blk - 6
                        buf, bb = qk[qi % 2], bqk[qi % 2]
                        if qi < 2:
                            self.act(buf[:], p[:], ACT.Identity, [bp], [bb], scale=ATTN_SCALE)
                            self.store(qk_ap(QT, qi, t), buf[:], [bb], [bQT[t]])
                        else:
                            self.copy("dve", buf[:], p[:], [bp], [bb])
                            self.store(qk_ap(KT, qi - 2, t), buf[:], [bb], [bKT[t]])
                for sub in range(4):
                    p, bp = PS[psi % 6], bPS[psi % 6]; psi += 1
                    for j in range(NCH):
                        self.mm(p[:, 0:256], hT[:, j, sub * 128:(sub + 1) * 128], wvs[:, j * 256:(j + 1) * 256],
                                j == 0, j == NCH - 1, [bhT[j], bwv], [bp])
                    i = sub % 2
                    self.copy("act" if sub % 2 else "dve", vs[i][:], p[:, 0:256], [bp], [bvs[i]])
                    r0 = (t * 4 + sub) * 128
                    self.store(VV[r0:r0 + 128, :], vs[i][:], [bvs[i]], [bVV[t]])
            self.tr.barrier()

        if self.dbg == "E1":
            return
        if self.dbg == "STOP":
            return
        with contextlib.ExitStack() as ph:
          if self.dbg != "noE2":
                kTs = self.sb(ph, "kTs", [128, 2, SEQ], BF16); bkT = [Buf() for _ in range(NT)]
                vvs = self.sb(ph, "vvs", [128, NT * 4, 256], BF16); bvv = [Buf() for _ in range(NT)]
                msk = self.sb(ph, "msk", [128, 4, TN]); bmsk = Buf()
                idf = self.sb(ph, "idf", [128, 128]); bidf = Buf()
                identb = self.sb(ph, "identb", [128, 128], BF16); bid = Buf()
                gsub = self.sb(ph, "gsub", [128, 256]); bgsub = Buf()
                lam4 = self.sb(ph, "lam4", [128, 512]); blam4 = Buf()
                pr = self.sb(ph, "pr", [128, 256]); bpr = Buf()
                sm2 = self.sb(ph, "sm2", [128, 4]); bsm2 = Buf()
                onesf = self.sb(ph, "onesf", [1, 128]); bonesf = Buf()
                nl = self.sb(ph, "nl", [128, 1]); bnl = Buf()
                PSs = [self.ps(ph, f"S{i}", [128, TN]) for i in range(3)]; bPSs = [Buf() for _ in range(3)]
                PTp = [self.ps(ph, f"PTp{i}", [128, TN], BF16) for i in range(2)]; bPTp = [Buf(), Buf()]
                PO = [self.ps(ph, f"PO{i}", [128, TN]) for i in range(2)]; bPO = [Buf(), Buf()]
                qs = [self.sb(ph, f"qs{i}", [128, 2, TN], BF16) for i in range(2)]; bqs = [Buf(), Buf()]
                mx = self.sb(ph, "mx", [128, NT]); bmx = Buf()
                rsum = self.sb(ph, "rsum", [128, NT]); brsum = Buf()
                negm = self.sb(ph, "negm", [128, 1]); bnegm = Buf()
                smk = [self.sb(ph, f"smk{i}", [128, TN]) for i in range(2)]; bsmk = [Buf(), Buf()]
                P = [self.sb(ph, f"P{i}", [128, TN], BF16) for i in range(2)]; bP = [Buf(), Buf()]
                PT = [self.sb(ph, f"PT{i}", [128, TN], BF16) for i in range(2)]; bPT = [Buf(), Buf()]
                rc = self.sb(ph, "rc", [128, 1]); brc = Buf()
                Om = [self.sb(ph, f"Om{i}", [128, 256]) for i in range(2)]; bOm = [Buf(), Buf()]
                oc = self.sb(ph, "oc", [128, 256]); boc = Buf()
                sqo = self.sb(ph, "sqo", [128, 256]); bsqo = Buf()
                ss = self.sb(ph, "ss", [128, 2]); bss = Buf()
                on = self.sb(ph, "on", [128, 256], BF16); bon = Buf()
                ya = [self.sb(ph, f"ya{i}", [128, 2, TN], BF16) for i in range(2)]; bya = [Buf(), Buf()]

                for t in range(NT):
                    for mp in range(2):
                        self.load(kTs[:, mp, t * TN:(t + 1) * TN], qk_ap(KT, mp, t), [bKT[t]], [bkT[t]])
                    for i4 in range(4):
                        r0 = (t * 4 + i4) * 128
                        self.load(vvs[:, t * 4 + i4, 0:256], VV[r0:r0 + 128, :], [bVV[t]], [bvv[t]])
                self.load(msk[:].rearrange("p r k -> p (r k)"), self.masks_in[:], [], [bmsk])
                self.load(idf[:], self.ident_in[:], [], [bidf])
                self.copy("dve", identb[:], idf[:], [bidf], [bid])
                self.load(gsub[:], W["sgr"][:], [], [bgsub])
                self.ts("dve", gsub[:], gsub[:], 1.0 - lam_init, None, ALU.mult, None, [bgsub], [bgsub])
                self.load(lam4[:], W["lam"][:], [], [blam4])
                self.tt("dve", pr[:, 0:128], lam4[:, 0:128], lam4[:, 128:256], ALU.mult, [blam4], [bpr])
                self.tt("dve", pr[:, 128:256], lam4[:, 256:384], lam4[:, 384:512], ALU.mult, [blam4, bpr], [bpr])
                self.reduce(sm2[:, 0:1], pr[:, 0:128], ALU.add, [bpr], [bsm2])
                self.reduce(sm2[:, 1:2], pr[:, 128:256], ALU.add, [bpr, bsm2], [bsm2])
                self.act(sm2[:, 0:2], sm2[:, 0:2], ACT.Exp, [bsm2], [bsm2])
                self.tt("dve", sm2[:, 2:3], sm2[:, 0:1], sm2[:, 1:2], ALU.subtract, [bsm2], [bsm2])
                self.ts("dve", sm2[:, 3:4], sm2[:, 2:3], lam_init, -1.0, ALU.add, ALU.mult, [bsm2], [bsm2])
                self.copy("dve", nl[:], sm2[:, 3:4], [bsm2], [bnl])

                cut = self.dbg[3] if (self.dbg or "").startswith("cut") else "Z"
                lastd = None
                si = 0
                pi = 0
                xi = 0
                import os
                for i in range(min(SEQ // 128, int(os.environ.get("LIMIT_I", "100000")))):
                    t, r = divmod(i, 4)
                    nkb = t + 1
                    wd = (r + 1) * 128
                    qb = t % 2
                    if r == 0:
                        for mp in range(2):
                            self.load(qs[qb][:, mp, :], qk_ap(QT, mp, t), [bQT[t]], [bqs[qb]])
                    for mp in range(2):
                        lhs = qs[qb][:, mp, r * 128:(r + 1) * 128]

                        def scores(kb):
                            nonlocal si
                            w = TN if kb < t else wd
                            ps, bps = PSs[si % 3], bPSs[si % 3]
                            si += 1
                            self.mm(ps[:, 0:w], lhs, kTs[:, mp, kb * TN:kb * TN + w], True, True,
                                    [bqs[qb], bkT[kb]], [bps])
                            if kb == t:
                                sk, bsk = smk[si % 2], bsmk[si % 2]
                                self.tt("dve", sk[:, 0:w], ps[:, 0:w], msk[:, r, 0:w], ALU.add, [bps, bmsk], [bsk])
                                return sk, bsk, w
                            return ps, bps, w

                        for kb in range(nkb):
                            src, bsrc, w = scores(kb)
                            self.reduce(mx[:, kb:kb + 1], src[:, 0:w], ALU.max, [bsrc, bmx], [bmx])
                        self.reduce(negm[:], mx[:, 0:nkb], ALU.max, [bmx], [bnegm])
                        self.ts("dve", negm[:], negm[:], -1.0, None, ALU.mult, None, [bnegm], [bnegm])
                        lastd = (negm[:], bnegm, 1)
                        if cut == "B":
                            continue
                        for kb in range(nkb):
                            src, bsrc, w = scores(kb)
                            Pb, bPb = P[pi % 2], bP[pi % 2]
                            PTb, bPTb = PT[pi % 2], bPT[pi % 2]
                            pi += 1
                            self.act(Pb[:, 0:w], src[:, 0:w], ACT.Exp, [bsrc, bnegm], [bPb], bias=negm[:, 0:1], scale=1.0)
                            self.reduce(rsum[:, kb:kb + 1], Pb[:, 0:w], ALU.add, [bPb, brsum], [brsum])
                            lastd = (Pb[:, 0:128], bPb, 128)
                            if cut == "C":
                                continue
                            nchunk = w // 128
                            ptp, bptp = PTp[xi % 2], bPTp[xi % 2]
                            xi += 1
                            for c in range(nchunk):
                                self.transpose(ptp[:, c * 128:(c + 1) * 128], Pb[:, c * 128:(c + 1) * 128], identb[:],
                                               [bPb, bid], [bptp], acc=(c > 0))
                            self.copy("act" if pi % 2 else "dve", PTb[:, 0:w], ptp[:, 0:w], [bptp], [bPTb])
                            lastd = (PTb[:, 0:128], bPTb, 128)
                            if cut == "D":
                                continue
                            KN = os.environ.get("KNOB", "")
                            for c in range(nchunk):
                                if KN == "K2" and c >= 2:
                                    continue
                                first = (kb == 0 and c == 0)
                                lastf = (kb == nkb - 1 and c == nchunk - 1)
                                if KN == "K2":
                                    lastf = (kb == nkb - 1 and c == min(nchunk, 2) - 1)
                                self.mm(PO[mp][:, 0:256], PTb[:, c * 128:(c + 1) * 128], vvs[:, kb * 4 + c, :],
                                        first, lastf, [bPTb, bvv[kb]], [bPO[mp]])
                        if cut in "CD":
                            continue
                        if os.environ.get("KNOB", "") == "K1" and i >= 2:
                            continue
                        self.reduce(rc[:], rsum[:, 0:nkb], ALU.add, [brsum], [brc])
                        self.recip(rc[:], rc[:], [brc], [brc])
                        self.ts("dve", Om[mp][:], PO[mp][:, 0:256], rc[:, 0:1], None, ALU.mult, None,
                                [bPO[mp], brc], [bOm[mp]])
                    if cut in "BCD":
                        continue
                    lastd = (Om[1][:], bOm[1], 256)
                    if cut == "E":
                        continue
                    self.stt(oc[:], Om[1][:], nl[:, 0:1], Om[0][:], ALU.mult, ALU.add, [bOm[0], bOm[1], bnl], [boc])
                    self.tt("dve", sqo[:], oc[:], oc[:], ALU.mult, [boc], [bsqo])
                    self.reduce(ss[:, 0:1], sqo[:], ALU.add, [bsqo, bss], [bss])
                    self.act(ss[:, 1:2], ss[:, 0:1], ACT.Sqrt, [bss, self.bconsts], [bss],
                             bias=self.consts[:, 2:3], scale=1.0 / 256.0)
                    self.recip(ss[:, 1:2], ss[:, 1:2], [bss], [bss])
                    self.stt(on[:], oc[:], ss[:, 1:2], gsub[:], ALU.mult, ALU.mult, [boc, bss, bgsub], [bon])
                    lastd = (on[:], bon, 256)
                    if cut == "F":
                        continue
                    ptp, bptp = PTp[xi % 2], bPTp[xi % 2]
                    xi += 1
                    for ec in range(2):
                        self.transpose(ptp[:, ec * 128:(ec + 1) * 128], on[:, ec * 128:(ec + 1) * 128], identb[:],
                                       [bon, bid], [bptp], acc=(ec > 0))
                    yb, byb = ya[t % 2], bya[t % 2]
                    for ec in range(2):
                        self.copy("act", yb[:, ec, r * 128:(r + 1) * 128], ptp[:, ec * 128:(ec + 1) * 128], [bptp], [byb])
                    if r == 3:
                        for ec in range(2):
                            self.store(ym_ap(t, 2 + ec), yb[:, ec, :], [byb], [bYM[t][2 + ec]], q="sp")
                if cut != "Z":
                    dap, dbuf, dn = lastd
                    self.copy("dve", sqo[:, 0:dn], dap, [dbuf], [bsqo])
                    self.store(self.out_ap[0:128, 0:dn], sqo[:, 0:dn], [bsqo], [Buf()])
                    self.tr.barrier()
                    self.dbg = "STOP"
                    return
                self.tr.barrier()

        if self.dbg == "dumpYM":
            with contextlib.ExitStack() as ph:
                yb_ = self.sb(ph, "dyb", [128, TN], BF16); byb_ = Buf()
                yf_ = self.sb(ph, "dyf", [128, TN]); byf_ = Buf()
                bo_ = Buf()
                for t in range(NT):
                    for ec in range(2):
                        self.load(yb_[:], ym_ap(t, 2 + ec), [bYM[t][2 + ec]], [byb_])
                        self.copy("dve", yf_[:], yb_[:], [byb_], [byf_])
                        r0 = (t * 2 + ec) * 128
                        self.store(self.out_ap[r0:r0 + 128, :], yf_[:], [byf_], [bo_])
                self.tr.barrier()
            self.dbg = "STOP"
            return
        if self.dbg == "E2":
            for nm_, T_ in (("dbg_ym", YM), ("dbg_qt", QT), ("dbg_kt", KT), ("dbg_vv", VV)):
                do = self.dout(nm_, list(T_.shape), BF16)
                self.store(do[:], T_[:], [], [Buf()])
            self.tr.barrier()
            return
        with contextlib.ExitStack() as ph:
            xt = self.sb(ph, "xt", [128, NCH, TN]); bxt = [Buf() for _ in range(NCH)]
            nbx = {"xt": xt, "bxt": bxt}
            wos = self.sb(ph, "wos", [128, NCH * 4 * 128], BF16); bwo = Buf()
            yms = [self.sb(ph, f"yms{i}", [128, 4, TN], BF16) for i in range(2)]; byms = [Buf(), Buf()]
            PS = [self.ps(ph, f"ps{i}", [128, TN]) for i in range(4)]; bPS = [Buf() for _ in range(4)]
            ep = self.alloc_ep(ph)
            for q4 in range(4):
                self.load(wos[:, q4 * 4096:(q4 + 1) * 4096], W["wo_b"][:, q4 * 4096:(q4 + 1) * 4096], [], [bwo])
            for t in range(NT):
                i = t % 2
                for k in range(4):
                    ksrc = (k % 2) if self.dbg == "expA" else k
                    self.load(yms[i][:, k, :], ym_ap(t, ksrc), [bYM[t][ksrc]], [byms[i]])
                for n in range(NCH):
                    p, bp = PS[n % 4], bPS[n % 4]
                    for k in range(4):
                        self.mm(p[:], wos[:, (n * 4 + k) * 128:(n * 4 + k + 1) * 128], yms[i][:, k, :],
                                k == 0, k == 3, [bwo, byms[i]], [bp])
                    self.epilogue(nbx, t, n, p, bp, g_ap, bVEC, ep)
            self.reduce_all(False, None, None)
            self.tr.barrier()


    def scan(self, out, d0, d1, init, reads, writes):
        self.tr.op("dve", lambda e: e.tensor_tensor_scan(out, d0, d1, init, ALU.mult, ALU.add), reads, writes)

    def sincos_args(self, x, bx, off, tmps, o_sin, bo_sin, o_cos, bo_cos):
        (xo, bxo), (q, bq), (qi, bqi), (r, br), (o1, bo1) = tmps
        TWO_PI = 2.0 * math.pi
        c1 = 6.28125
        rem = TWO_PI - c1
        c2 = float(np.array([np.float32(rem).view(np.uint32) & np.uint32(0xFFFFF000)], dtype=np.uint32).view(np.float32)[0])
        c3 = rem - c2
        self.ts("dve", xo, x, off, None, ALU.add, None, [bx], [bxo])
        self.ts("dve", q, xo, 1.0 / TWO_PI, None, ALU.mult, None, [bxo], [bq])
        self.copy("dve", qi, q, [bq], [bqi])
        self.copy("dve", q, qi, [bqi], [bq])
        self.stt(r, q, -c1, xo, ALU.mult, ALU.add, [bq, bxo], [br])
        self.stt(r, q, -c2, r, ALU.mult, ALU.add, [bq, br], [br])
        self.stt(r, q, -c3, r, ALU.mult, ALU.add, [bq, br], [br])
        self.ts("dve", o1, r, math.pi, -TWO_PI, ALU.is_gt, ALU.mult, [br], [bo1])
        self.tt("dve", o_sin, r, o1, ALU.add, [br, bo1], [bo_sin])
        self.ts("dve", o1, r, math.pi / 2, -TWO_PI, ALU.is_gt, ALU.mult, [br], [bo1])
        self.stt(o_cos, r, math.pi / 2, o1, ALU.add, ALU.add, [br, bo1], [bo_cos])

    def odd_mixer(self, l, W):
        NT, SEQ, TPS = self.NT, self.SEQ, self.TPS
        VEC, bVEC = self.VEC, self.bVEC
        o = l * 6 * NCH
        A_ap = VEC[:, o:o + NCH]; sh_ap = VEC[:, o + NCH:o + 2 * NCH]
        UC, GC, GALL, YC = self.UC, self.GC, self.GALL, self.YC
        bUC = [[Buf() for _ in range(NT)] for _ in range(4)]
        bGC = [[Buf() for _ in range(4)] for _ in range(NT)]
        bYC = [[Buf() for _ in range(4)] for _ in range(NT)]
        bGALL = Buf()
        TWO_PI = 2.0 * math.pi

        def uc_ap(k, t):
            r0 = (k * NT + t) * 128
            return UC[r0:r0 + 128, :]

        def gc_row(t, k):
            return (t * 4 + k) * 128

        with contextlib.ExitStack() as ph:
            nb = self.alloc_norm_bufs(ph)
            hT, bhT = nb["hT"], nb["bhT"]
            PS = [self.ps(ph, f"ps{i}", [128, TN]) for i in range(4)]; bPS = [Buf() for _ in range(4)]
            sels = self.sb(ph, "sels", [128, 4, D], BF16); bsel = Buf()
            us = [self.sb(ph, f"us{i}", [128, TN], BF16) for i in range(2)]; bus = [Buf(), Buf()]
            for k in range(4):
                self.load(sels[:, k, :], self.sel_b[k * 128:(k + 1) * 128, :], [], [bsel])
            it = 0
            pend, ybufs = self.take_pending(ph)
            for t in range(NT):
                self.norm_tile(nb, t, A_ap, sh_ap, [bVEC], src=self.fused_src(nb, t, pend, ybufs) if pend else None)
                for k in range(4):
                    p, bp = PS[k], bPS[k]
                    for j in range(NCH):
                        self.mm(p[:], sels[:, k, j * 128:(j + 1) * 128], hT[:, j, :], j == 0, j == NCH - 1,
                                [bsel, bhT[j]], [bp])
                    i = it % 2; it += 1
                    self.copy("act" if k % 2 else "dve", us[i][:], p[:], [bp], [bus[i]])
                    self.store(uc_ap(k, t), us[i][:], [bus[i]], [bUC[k][t]])
            self.tr.barrier()

        with contextlib.ExitStack() as ph:
            def T(name, shape, dt=F32):
                return self.sb(ph, name, shape, dt), Buf()
            are, bare = T("are", [128, 16]); aim, baim = T("aim", [128, 16]); ldt, bldt = T("ldt", [128, 16])
            dt_, bdt = T("dt", [128, 16]); mag, bmag = T("mag", [128, 16]); th, bth = T("th", [128, 16])
            thr, bthr = T("thr", [128, 16]); sn, bsn = T("sn", [128, 16]); cs, bcs = T("cs", [128, 16])
            w1, bw1 = T("w1", [128, 16]); w2, bw2 = T("w2", [128, 16]); w3, bw3 = T("w3", [128, 16])
            fre, bfre = T("fre", [128, 16]); fim, bfim = T("fim", [128, 16])
            Bre, bBre = T("Bre", [128, 16, 128], BF16); Bim, bBim = T("Bim", [128, 16, 128], BF16)
            Cre, bCre = T("Cre", [128, 16, 128]); Cim, bCim = T("Cim", [128, 16, 128])
            dsk, bdsk = T("dsk", [128, 4]); J, bJ = T("J", [128, S5L])
            ang, bang = T("ang", [128, S5L]); cosT, bcos = T("cosT", [128, S5L]); sinT, bsin = T("sinT", [128, S5L])
            Ere, bEre = T("Ere", [128, S5L]); Eim, bEim = T("Eim", [128, S5L]); MAGT, bMAGT = T("MAGT", [128, S5L])
            tq, btq = T("tq", [128, S5L])
            uT = [T(f"uT{i}", [128, S5L], BF16) for i in range(2)]
            xr = [T(f"xr{i}", [128, S5L]) for i in range(2)]; xi_ = [T(f"xi{i}", [128, S5L]) for i in range(2)]
            a1 = T("a1", [128, S5L]); a2 = T("a2", [128, S5L]); b1 = T("b1", [128, S5L]); b2 = T("b2", [128, S5L])
            brp = [T(f"brp{i}", [128, S5L]) for i in range(2)]; bip = [T(f"bip{i}", [128, S5L]) for i in range(2)]
            srp = T("srp", [128, S5L]); sip = T("sip", [128, S5L])
            c1 = T("c1", [128, S5L]); c2 = T("c2", [128, S5L]); c3 = T("c3", [128, S5L]); c4 = T("c4", [128, S5L])
            SR = [T(f"SR{i}", [128, S5L]) for i in range(2)]; SI = [T(f"SI{i}", [128, S5L]) for i in range(2)]
            yy = T("yy", [128, S5L]); e1 = T("e1", [128, S5L]); e2 = T("e2", [128, S5L]); sg = T("sg", [128, S5L])
            gg = [T(f"gg{i}", [128, S5L], BF16) for i in range(2)]
            PX = [(self.ps(ph, f"px{i}", [128, TN]), Buf()) for i in range(4)]
            PY = [(self.ps(ph, f"py{i}", [128, TN]), Buf()) for i in range(2)]

            self.load(are[:], W["are"][:], [], [bare]); self.load(aim[:], W["aim"][:], [], [baim])
            self.load(ldt[:], W["ldt"][:], [], [bldt]); self.load(dsk[:], W["dsk"][:], [], [bdsk])
            self.load(J[:], self.iota_in[:], [], [bJ])
            self.load(Bre[:].rearrange("p g c -> p (g c)"), W["bre_b"][:], [], [bBre])
            self.load(Bim[:].rearrange("p g c -> p (g c)"), W["bim_b"][:], [], [bBim])
            self.load(Cre[:].rearrange("p g c -> p (g c)"), W["cre"][:], [], [bCre])
            self.load(Cim[:].rearrange("p g c -> p (g c)"), W["cim"][:], [], [bCim])
            self.ts("dve", Cim[:].rearrange("p g c -> p (g c)"), Cim[:].rearrange("p g c -> p (g c)"), -1.0, None,
                    ALU.mult, None, [bCim], [bCim])
            self.ts("dve", are[:], are[:], -1e-4, None, ALU.min, None, [bare], [bare])
            self.act(dt_[:], ldt[:], ACT.Exp, [bldt], [bdt])
            self.tt("dve", w1[:], are[:], dt_[:], ALU.mult, [bare, bdt], [bw1])
            self.act(mag[:], w1[:], ACT.Exp, [bw1], [bmag])
            self.tt("dve", th[:], aim[:], dt_[:], ALU.mult, [baim, bdt], [bth])
            I32 = mybir.dt.int32
            tm16 = [T("t16a", [128, 16]), T("t16b", [128, 16]), T("t16i", [128, 16], I32), T("t16c", [128, 16]),
                    T("t16d", [128, 16])]
            tm16 = [(a[:], b) for a, b in tm16]
            tmL = [T("tLa", [128, S5L]), T("tLb", [128, S5L]), T("tLi", [128, S5L], I32), T("tLc", [128, S5L]),
                   T("tLd", [128, S5L])]
            tmL = [(a[:], b) for a, b in tmL]
            self.sincos_args(th[:], bth, 16.0 * math.pi, tm16, thr[:], bthr, w2[:], bw2)
            self.act(sn[:], thr[:], ACT.Sin, [bthr], [bsn])
            self.act(cs[:], w2[:], ACT.Sin, [bw2], [bcs])
            self.tt("dve", cs[:], cs[:], mag[:], ALU.mult, [bcs, bmag], [bcs])
            self.ts("dve", cs[:], cs[:], -1.0, None, ALU.add, None, [bcs], [bcs])
            self.tt("dve", sn[:], sn[:], mag[:], ALU.mult, [bsn, bmag], [bsn])
            self.tt("dve", w1[:], are[:], are[:], ALU.mult, [bare], [bw1])
            self.tt("dve", w2[:], aim[:], aim[:], ALU.mult, [baim], [bw2])
            self.tt("dve", w1[:], w1[:], w2[:], ALU.add, [bw1, bw2], [bw1])
            self.recip(w1[:], w1[:], [bw1], [bw1])
            self.tt("dve", w2[:], cs[:], are[:], ALU.mult, [bcs, bare], [bw2])
            self.tt("dve", w3[:], sn[:], aim[:], ALU.mult, [bsn, baim], [bw3])
            self.tt("dve", w2[:], w2[:], w3[:], ALU.add, [bw2, bw3], [bw2])
            self.tt("dve", fre[:], w2[:], w1[:], ALU.mult, [bw2, bw1], [bfre])
            self.tt("dve", w2[:], sn[:], are[:], ALU.mult, [bsn, bare], [bw2])
            self.tt("dve", w3[:], cs[:], aim[:], ALU.mult, [bcs, baim], [bw3])
            self.tt("dve", w2[:], w2[:], w3[:], ALU.subtract, [bw2, bw3], [bw2])
            self.tt("dve", fim[:], w2[:], w1[:], ALU.mult, [bw2, bw1], [bfim])

            xc = 0
            for gp in range(16):
                k = gp // 4
                R0 = 32 * (gp % 4)
                self.ts("dve", ang[:], J[:], thr[:, gp:gp + 1], None, ALU.mult, None, [bJ, bthr], [bang])
                self.sincos_args(ang[:], bang, 520.0 * math.pi, tmL, tq[:], btq, MAGT[:], bMAGT)
                self.act(sinT[:], tq[:], ACT.Sin, [btq], [bsin])
                self.act(cosT[:], MAGT[:], ACT.Sin, [bMAGT], [bcos])
                self.ts("dve", tq[:], cosT[:], fre[:, gp:gp + 1], None, ALU.mult, None, [bcos, bfre], [btq])
                self.stt(Ere[:], sinT[:], fim[:, gp:gp + 1], tq[:], ALU.mult, ALU.add, [bsin, bfim, btq], [bEre])
                self.ts("dve", tq[:], sinT[:], fre[:, gp:gp + 1], None, ALU.mult, None, [bsin, bfre], [btq])
                self.stt(Eim[:], cosT[:], fim[:, gp:gp + 1], tq[:], ALU.mult, ALU.subtract, [bcos, bfim, btq], [bEim])
                self.ts("dve", MAGT[:], J[:], 0.0, mag[:, gp:gp + 1], ALU.mult, ALU.add, [bJ, bmag], [bMAGT])
                for t in range(NT):
                    i = xc % 2; xc += 1
                    u_, bu_ = uT[i]
                    self.load(u_[:], uc_ap(k, t), [bUC[k][t]], [bu_])
                    (pxr, bpxr), (pxi, bpxi) = PX[2 * i], PX[2 * i + 1]
                    self.mm(pxr[:], Bre[:, gp, :], u_[:], True, True, [bBre, bu_], [bpxr])
                    self.mm(pxi[:], Bim[:, gp, :], u_[:], True, True, [bBim, bu_], [bpxi])
                    (xr_, bxr_), (xi2, bxi2) = xr[i], xi_[i]
                    self.copy("act", xr_[:], pxr[:], [bpxr], [bxr_])
                    self.copy("act", xi2[:], pxi[:], [bpxi], [bxi2])
                    self.tt("pool", a1[0][:], xr_[:], Ere[:], ALU.mult, [bxr_, bEre], [a1[1]])
                    self.tt("pool", a2[0][:], xi2[:], Eim[:], ALU.mult, [bxi2, bEim], [a2[1]])
                    (brp_, bbrp_), (bip_, bbip_) = brp[i], bip[i]
                    self.tt("dve", brp_[:], a1[0][:], a2[0][:], ALU.subtract, [a1[1], a2[1]], [bbrp_])
                    self.tt("pool", b1[0][:], xr_[:], Eim[:], ALU.mult, [bxr_, bEim], [b1[1]])
                    self.tt("pool", b2[0][:], xi2[:], Ere[:], ALU.mult, [bxi2, bEre], [b2[1]])
                    self.tt("dve", bip_[:], b1[0][:], b2[0][:], ALU.add, [b1[1], b2[1]], [bbip_])
                    if t == 0:
                        ir, ii, rd = 0.0, 0.0, []
                    else:
                        (psr, bpsr), (psi_, bpsi_) = SR[1 - i], SI[1 - i]
                        ir, ii, rd = psr[:, S5L - 1:S5L], psi_[:, S5L - 1:S5L], [bpsr, bpsi_]
                    self.scan(srp[0][:], MAGT[:], brp_[:], ir, [bMAGT, bbrp_] + rd, [srp[1]])
                    self.scan(sip[0][:], MAGT[:], bip_[:], ii, [bMAGT, bbip_] + rd, [sip[1]])
                    (SR_, bSR_), (SI_, bSI_) = SR[i], SI[i]
                    self.tt("pool", c1[0][:], srp[0][:], cosT[:], ALU.mult, [srp[1], bcos], [c1[1]])
                    self.tt("dve", c2[0][:], sip[0][:], sinT[:], ALU.mult, [sip[1], bsin], [c2[1]])
                    self.tt("dve", SR_[:], c1[0][:], c2[0][:], ALU.subtract, [c1[1], c2[1]], [bSR_])
                    self.tt("pool", c3[0][:], srp[0][:], sinT[:], ALU.mult, [srp[1], bsin], [c3[1]])
                    self.tt("dve", c4[0][:], sip[0][:], cosT[:], ALU.mult, [sip[1], bcos], [c4[1]])
                    self.tt("pool", SI_[:], c3[0][:], c4[0][:], ALU.add, [c3[1], c4[1]], [bSI_])
                    py, bpy = PY[i]
                    self.mm(py[:], Cre[:, gp, :], SR_[:], True, False, [bCre, bSR_], [bpy])
                    self.mm(py[:], Cim[:, gp, :], SI_[:], False, True, [bCim, bSI_], [bpy])
                    rs = slice(R0, R0 + 32)
                    self.stt(yy[0][rs, :], u_[rs, :], dsk[rs, k:k + 1], py[rs, :], ALU.mult, ALU.add,
                             [bu_, bdsk, bpy], [yy[1]])
                    self.tt("pool", e1[0][rs, :], yy[0][rs, :], yy[0][rs, :], ALU.mult, [yy[1]], [e1[1]])
                    self.ts("pool", e1[0][rs, :], e1[0][rs, :], 0.044715, 1.0, ALU.mult, ALU.add, [e1[1]], [e1[1]])
                    self.tt("pool", e2[0][rs, :], e1[0][rs, :], yy[0][rs, :], ALU.mult, [e1[1], yy[1]], [e2[1]])
                    self.act(sg[0][rs, :], e2[0][rs, :], ACT.Sigmoid, [e2[1]], [sg[1]], scale=1.5957691216057308)
                    g_, bg_ = gg[i]
                    self.tt("dve", g_[rs, :], yy[0][rs, :], sg[0][rs, :], ALU.mult, [yy[1], sg[1]], [bg_])
                    r0 = gc_row(t, k) + R0
                    self.store(GC[r0:r0 + 32, :], g_[rs, :], [bg_], [bGC[t][k]], q="sp")
            self.tr.barrier()

        NG = max(1, NT // 16)
        TG = NT // NG
        GR = TG * 4 * 128

        def gall_row(r, t, kk):
            g = t // TG
            return ((((g * NCORE + r) * TG) + t % TG) * 4 + kk) * 128
        self.gall_row = gall_row
        for g in range(NG):
            rbs = [b for t in range(g * TG, (g + 1) * TG) for b in bGC[t]]
            self.collective("AllGather", ALU.bypass, GC[g * GR:(g + 1) * GR, :],
                            GALL[g * NCORE * GR:(g + 1) * NCORE * GR, :], rbs, [bGALL])

        bYALL = Buf()
        with contextlib.ExitStack() as ph:
            w1s = self.sb(ph, "w1s", [128, 4, D], BF16); bw1s = Buf()
            w2s = self.sb(ph, "w2s", [128, 4, D], BF16); bw2s = Buf()
            gT = [self.sb(ph, f"gT{i}", [128, NCH, TN], BF16) for i in range(2)]
            bgT = [[Buf() for _ in range(NCH)] for _ in range(2)]
            PS = [self.ps(ph, f"ps{i}", [128, TN]) for i in range(4)]; bPS = [Buf() for _ in range(4)]
            sgm = [self.sb(ph, f"sgm{i}", [128, TN]) for i in range(2)]; bsgm = [Buf(), Buf()]
            ys = [self.sb(ph, f"ys{i}", [128, TN], BF16) for i in range(2)]; bys = [Buf(), Buf()]
            for k in range(4):
                self.load(w1s[:, k, :], W["w1_b"][k * 128:(k + 1) * 128, :], [], [bw1s])
                self.load(w2s[:, k, :], W["w2_b"][k * 128:(k + 1) * 128, :], [], [bw2s])
            it = 0
            for t in range(NT):
                gi = t % 2
                for j in range(NCH):
                    r, kk = divmod(j, 4)
                    r0 = gall_row(r, t, kk)
                    self.load(gT[gi][:, j, :], GALL[r0:r0 + 128, :], [bGALL], [bgT[gi][j]])
                for k in range(4):
                    i = it % 2; it += 1
                    p1, bp1 = PS[2 * i], bPS[2 * i]
                    p2, bp2 = PS[2 * i + 1], bPS[2 * i + 1]
                    for j in range(NCH):
                        self.mm(p1[:], w1s[:, k, j * 128:(j + 1) * 128], gT[gi][:, j, :], j == 0, j == NCH - 1,
                                [bw1s, bgT[gi][j]], [bp1])
                    for j in range(NCH):
                        self.mm(p2[:], w2s[:, k, j * 128:(j + 1) * 128], gT[gi][:, j, :], j == 0, j == NCH - 1,
                                [bw2s, bgT[gi][j]], [bp2])
                    self.act(sgm[i][:], p2[:], ACT.Sigmoid, [bp2], [bsgm[i]])
                    self.tt("dve", ys[i][:], sgm[i][:], p1[:], ALU.mult, [bsgm[i], bp1], [bys[i]])
                    r0 = gc_row(t, k)
                    self.store(YC[r0:r0 + 128, :], ys[i][:], [bys[i]], [bYC[t][k]])
            self.tr.barrier()
        for g in range(NG):
            rbs = [b for t in range(g * TG, (g + 1) * TG) for b in bYC[t]]
            self.collective("AllGather", ALU.bypass, YC[g * GR:(g + 1) * GR, :],
                            self.YALL[g * NCORE * GR:(g + 1) * NCORE * GR, :], rbs, [bYALL])
        self.pending = ("glu", bYALL, VEC[:, o + 2 * NCH:o + 3 * NCH])
        return None

    def final_norm(self, XF, bXF, out, AFIN, bAFIN):
        TPS = self.TPS
        if not self.do_final:
            self.store(out[:], XF[:], [bXF], [Buf()])
            return
        with contextlib.ExitStack() as ph:
            nb = self.alloc_norm_bufs(ph, F32)
            bout = Buf()
            xt, bxt = nb["xt"], nb["bxt"]
            for t in range(TPS):
                def src(j, t=t):
                    r0 = (t * NCH + j) * 128
                    self.load(xt[:, j, :], XF[r0:r0 + 128, :], [bXF], [bxt[j]])
                self.norm_tile(nb, t, AFIN, self.ZSH, [bAFIN, self.bZSH], src=src)
                for j in range(NCH):
                    r0 = (t * NCH + j) * 128
                    self.store(out[r0:r0 + 128, :], nb["hT"][:, j, :], [nb["bhT"][j]], [bout])

    def emit(self):
        nc = self.nc
        tr = self.tr
        with contextlib.ExitStack() as es:
            sems = [es.enter_context(nc.semaphore(f"s{i}")) for i in range(tr.nsem)]
            block = es.enter_context(nc.Block())

            def replay(stream, e):
                for op in stream.ops:
                    if op[0] == "w":
                        e.wait_ge(sems[op[1]], op[2])
                    else:
                        op[1](e).then_inc(sems[op[2]], op[3])

            S = tr.streams

            @block.tensor
            def _(e):
                replay(S["pe"], e)

            @block.scalar
            def _(e):
                replay(S["act"], e)

            @block.vector
            def _(e):
                replay(S["dve"], e)

            @block.gpsimd
            def _(e):
                replay(S["pool"], e)

            @block.sync
            def _(e):
                replay(S["sp"], e)


def self_ap(t):
    return t[:]


def _tiles(xm):
    S = xm.shape[0]
    return np.ascontiguousarray(xm.reshape(S // TN, TN, NCH, 128).transpose(0, 2, 3, 1))


def _pj(v):
    lead = int(np.prod(v.shape[:-1])) if v.ndim > 1 else 1
    return np.ascontiguousarray(v.reshape(lead, NCH, 128).transpose(2, 0, 1).reshape(128, lead * NCH))


def prep_inputs(inp, SEQ, DEPTH):
    NT = SEQ // TN
    TPS = NT // NCORE
    maps = [dict() for _ in range(NCORE)]
    xt = _tiles(np.asarray(inp["x"])[0, :SEQ])
    w_ada = np.asarray(inp["w_ada"]); b_ada = np.asarray(inp["b_ada"])
    common = {
        "c_in": np.ascontiguousarray(np.asarray(inp["c"])[0].reshape(NCH, 128).T),
        "adat": _pj(np.asarray(inp["ada_table"])[:DEPTH]),
        "nmix": _pj(np.asarray(inp["norm_mix"])[:DEPTH]),
        "nffn": _pj(np.asarray(inp["norm_ffn"])[:DEPTH]),
        "nfin": _pj(np.asarray(inp["norm_final"])),
    }
    wg = np.asarray(inp["ffn_w_gate"]); wu = np.asarray(inp["ffn_w_up"]); wd = np.asarray(inp["ffn_w_down"])
    for c in range(NCORE):
        m = maps[c]
        m.update(common)
        m["x_in"] = xt[c * TPS:(c + 1) * TPS].reshape(TPS * NCH * 128, TN)
        wa = w_ada[:, 3072 * c:3072 * (c + 1)].reshape(NCH, 128, 24, 128).transpose(2, 1, 0, 3)
        m["wada"] = np.ascontiguousarray(wa).reshape(24, 128, NCH * 128)
        m["bada"] = np.ascontiguousarray(b_ada[3072 * c:3072 * (c + 1)].reshape(24, 128).T)
    for l in range(DEPTH):
        wgp = np.zeros((D, HPAD), np.float32); wgp[:, :FFN_H] = wg[l]
        wup = np.zeros((D, HPAD), np.float32); wup[:, :FFN_H] = wu[l]
        wdp = np.zeros((HPAD, D), np.float32); wdp[:FFN_H] = wd[l]
        for c in range(NCORE):
            h0 = HCH * 128 * c
            a = wgp[:, h0:h0 + HCH * 128].reshape(NCH, 128, HCH, 128).transpose(2, 1, 0, 3)
            maps[c][f"wg{l}"] = np.ascontiguousarray(a).reshape(HCH * 128, D)
            a = wup[:, h0:h0 + HCH * 128].reshape(NCH, 128, HCH, 128).transpose(2, 1, 0, 3)
            maps[c][f"wu{l}"] = np.ascontiguousarray(a).reshape(HCH * 128, D)
            a = wdp[h0:h0 + HCH * 128].reshape(HCH, 128, NCH, 128).transpose(2, 1, 0, 3)
            maps[c][f"wd{l}"] = np.ascontiguousarray(a).reshape(NCH * 128, HCH * 128)
    return maps


def prep_even(inp, maps, DEPTH):
    NE = (DEPTH + 1) // 2
    qq = np.arange(128)[:, None, None]; rr = np.arange(4)[None, :, None]; kk = np.arange(TN)[None, None, :]
    masks = np.where(kk <= rr * 128 + qq, 0.0, -30000.0).astype(np.float32).reshape(128, 4 * TN)
    ident = np.eye(128, dtype=np.float32)
    w_in = np.asarray(inp["mix_w_in"]); w_out = np.asarray(inp["mix_w_out"]); conv_w = np.asarray(inp["conv_w"])
    for c in range(NCORE):
        m = maps[c]
        m["masks"] = masks; m["ident"] = ident
        for e in range(NE):
            col0 = []
            for ch in range(2):
                col0 += [256 * c + ch * 128, 2048 + 256 * c + ch * 128, 4096 + 256 * c + ch * 128]
            col0 += [6144 + 256 * c, 6144 + 256 * c + 128, 8192 + 256 * c, 8192 + 256 * c + 128]
            blks = [w_in[e][:, c0:c0 + 128].reshape(NCH, 128, 128).transpose(1, 0, 2).reshape(128, D) for c0 in col0]
            m[f"wi{e}"] = np.ascontiguousarray(np.concatenate(blks, 0))
            v0 = 10240 + 256 * c
            m[f"wv{e}"] = np.ascontiguousarray(
                w_in[e][:, v0:v0 + 256].reshape(NCH, 128, 256).transpose(1, 0, 2).reshape(128, NCH * 256))
            rows = np.concatenate([np.arange(256 * c, 256 * c + 256), np.arange(2048 + 256 * c, 2048 + 256 * c + 256)])
            wsel = w_out[e][rows].reshape(4, 128, NCH, 128).transpose(1, 2, 0, 3)
            m[f"wo{e}"] = np.ascontiguousarray(wsel).reshape(128, NCH * 4 * 128)
            cw = [conv_w[e][:, 256 * c + ch * 128:256 * c + (ch + 1) * 128].T for ch in range(2)]
            m[f"cw{e}"] = np.ascontiguousarray(np.concatenate(cw, 1))
            m[f"lam{e}"] = np.concatenate([np.asarray(inp[k])[e] for k in
                                           ("lambda_q1", "lambda_k1", "lambda_q2", "lambda_k2")]).reshape(1, 512).astype(np.float32).repeat(128, axis=0)
            m[f"sgr{e}"] = np.ascontiguousarray(np.tile(np.asarray(inp["subln_g"])[e][None, :], (128, 1)))


def prep_odd(inp, maps, DEPTH):
    NO = DEPTH // 2
    iota = np.ascontiguousarray(np.tile(np.arange(1, S5L + 1, dtype=np.float32)[None, :], (128, 1)))
    a_re = np.asarray(inp["s5_a_re"]); a_im = np.asarray(inp["s5_a_im"]); log_dt = np.asarray(inp["s5_log_dt"])
    b_re = np.asarray(inp["s5_b_re"]); b_im = np.asarray(inp["s5_b_im"])
    c_re = np.asarray(inp["s5_c_re"]); c_im = np.asarray(inp["s5_c_im"]); d_skip = np.asarray(inp["s5_d"])
    w1 = np.asarray(inp["glu_w1"]); w2 = np.asarray(inp["glu_w2"])
    eye = np.eye(128, dtype=np.float32)
    for c in range(NCORE):
        m = maps[c]
        m["iota"] = iota
        sel = np.zeros((4, 128, NCH, 128), np.float32)
        for k in range(4):
            sel[k, :, 4 * c + k, :] = eye
        m["sel"] = sel.reshape(4 * 128, D)
        for o in range(NO):
            g0 = 32 * c
            def pg(a):
                return np.ascontiguousarray(a[g0:g0 + 32].reshape(16, 2, 64).transpose(1, 2, 0).reshape(128, 16))
            m[f"are{o}"] = pg(a_re[o]); m[f"aim{o}"] = pg(a_im[o])
            m[f"ldt{o}"] = pg(np.repeat(log_dt[o][:, None], 64, axis=1))
            m[f"dsk{o}"] = np.ascontiguousarray(d_skip[o][512 * c:512 * (c + 1)].reshape(4, 128).T)
            bre = np.zeros((128, 16, 128), np.float32); bim = np.zeros((128, 16, 128), np.float32)
            cre = np.zeros((128, 16, 128), np.float32); cim = np.zeros((128, 16, 128), np.float32)
            for gp in range(16):
                for gi in range(2):
                    g = g0 + 2 * gp + gi
                    r0 = 32 * (gp % 4) + 16 * gi
                    bre[r0:r0 + 16, gp, 64 * gi:64 * gi + 64] = b_re[o][g].T
                    bim[r0:r0 + 16, gp, 64 * gi:64 * gi + 64] = b_im[o][g].T
                    cre[64 * gi:64 * gi + 64, gp, r0:r0 + 16] = c_re[o][g].T
                    cim[64 * gi:64 * gi + 64, gp, r0:r0 + 16] = c_im[o][g].T
            m[f"bre{o}"] = bre.reshape(128, 2048); m[f"bim{o}"] = bim.reshape(128, 2048)
            m[f"cre{o}"] = cre.reshape(128, 2048); m[f"cim{o}"] = cim.reshape(128, 2048)
            for nm_, w in (("gw1", w1), ("gw2", w2)):
                a = w[o][:, 512 * c:512 * (c + 1)].reshape(NCH, 128, 4, 128).transpose(2, 1, 0, 3)
                m[f"{nm_}{o}"] = np.ascontiguousarray(a).reshape(4 * 128, D)


def assemble(results, SEQ):
    NT = SEQ // TN
    TPS = NT // NCORE
    outs = [np.asarray(r["out"]).reshape(TPS, NCH, 128, TN) for r in results]
    o = np.concatenate(outs, 0)
    return np.ascontiguousarray(o.transpose(0, 3, 1, 2)).reshape(1, SEQ, D)


def run(inp, SEQ, DEPTH, do_mix=True, trace=False, dbg=None, l0=0, do_final=True):
    prog = Prog(SEQ, DEPTH, do_mix, dbg, l0, do_final)
    nc = prog.build()
    maps = prep_inputs(inp, SEQ, DEPTH)
    if do_mix:
        prep_even(inp, maps, DEPTH)
        if DEPTH >= 2:
            prep_odd(inp, maps, DEPTH)
    maps = [{k: v for k, v in m.items() if k in prog.in_names} for m in maps]
    res = run_bass_kernel_spmd(nc, maps, core_ids=list(range(NCORE)), trace=trace)
    if dbg and dbg not in ("noE2", "oddonly", "expA"):
        return None, res
    return assemble(res.results, SEQ), res


_PER_LAYER = ("ada_table", "norm_mix", "norm_ffn", "ffn_w_gate", "ffn_w_up", "ffn_w_down")
_PER_EVEN = ("mix_w_in", "conv_w", "lambda_q1", "lambda_k1", "lambda_q2", "lambda_k2", "subln_g", "mix_w_out")
_PER_ODD = ("s5_a_re", "s5_a_im", "s5_log_dt", "s5_b_re", "s5_b_im", "s5_c_re", "s5_c_im", "s5_d", "glu_w1", "glu_w2")


def _slice_layers(inputs, l0, n):
    d = {k: np.asarray(v) for k, v in inputs.items()}
    for k in _PER_LAYER:
        d[k] = d[k][l0:l0 + n]
    for k in _PER_EVEN:
        d[k] = d[k][l0 // 2:(l0 + n + 1) // 2]
    for k in _PER_ODD:
        d[k] = d[k][l0 // 2:(l0 + n) // 2]
    return d


def kernel(**inputs):
    out, _ = run(inputs, 16384, 4, True)
    return out.astype(np.float32)
```
